# Optimizing a Trainium2 kernel written in Bass

```python
import jax, jax.numpy as jnp
from jax import lax
import numpy as np

D_MODEL = 1024
BATCH = 8
SEQ = 8192
DEPTH = 1
DEC_BATCH = 16
DEC_SEQ = 16
PAST_LEN = 1024

CHUNK = 64
MIX_WIDTH = D_MODEL
HEAD_DIM = 64
RWKV_WIDTH = MIX_WIDTH // 2
RWKV_HEADS = RWKV_WIDTH // HEAD_DIM
ATT_WIDTH = MIX_WIDTH - RWKV_WIDTH
ATT_HEADS = ATT_WIDTH // HEAD_DIM
DECAY_LORA = 64
ICLR_LORA = 64
GATE_LORA = 128
RWKV_COLS = 3 * RWKV_WIDTH + DECAY_LORA + ICLR_LORA + GATE_LORA
RWKV_SPLITS = [RWKV_WIDTH, 2 * RWKV_WIDTH, 3 * RWKV_WIDTH,
               3 * RWKV_WIDTH + DECAY_LORA, 3 * RWKV_WIDTH + DECAY_LORA + ICLR_LORA]
ATT_COLS = 3 * ATT_WIDTH
IN_COLS = RWKV_COLS + ATT_COLS
LEFT_CHUNKS = 8
ATT_WINDOW = LEFT_CHUNKS * CHUNK
BAND = ATT_WINDOW + CHUNK
REL_MAX = 128
N_REL = 2 * REL_MAX + 1
D_FF = 2816
NORM_EPS = 1e-5
GN_EPS = 64e-5
NEG_INF = -1e30

kernel_name = "rwkv7_chunkband_hymba_macaron_step"


def _rms_norm(x, g):
    xf = x.astype(jnp.float32)
    xf = xf * lax.rsqrt(jnp.mean(xf * xf, axis=-1, keepdims=True) + NORM_EPS)
    return (xf * g.astype(jnp.float32)).astype(x.dtype)


def _swiglu(x, w_gate, w_up, w_down):
    return (jax.nn.silu(x @ w_gate) * (x @ w_up)) @ w_down


def _wkv_scan(r, decay, k, v, kk, a, s0):
    def step(s, inp):
        r_t, w_t, k_t, v_t, kk_t, a_t = inp
        sa = jnp.einsum('bhvk,bhk->bhv', s, kk_t)
        s = (s * w_t[:, :, None, :] - sa[..., None] * (kk_t * a_t)[:, :, None, :]
             + v_t[..., None] * k_t[:, :, None, :])
        y_t = jnp.einsum('bhvk,bhk->bhv', s, r_t)
        return s, y_t
    xs = (jnp.moveaxis(r, 1, 0), jnp.moveaxis(decay, 1, 0), jnp.moveaxis(k, 1, 0),
          jnp.moveaxis(v, 1, 0), jnp.moveaxis(kk, 1, 0), jnp.moveaxis(a, 1, 0))
    s, ys = lax.scan(step, s0, xs)
    return jnp.moveaxis(ys, 0, 1), s


def _rwkv7(p, shift_prev, wkv_prev, mu_shift, w0, w_lora_up, a0, a_lora_up, g_lora_up,
           k_k, k_a, r_k, ln_x_w, ln_x_b):
    B, T, _ = p.shape
    f32 = jnp.float32
    prev = jnp.concatenate([shift_prev[:, None, :].astype(p.dtype), p[:, :-1]], axis=1)
    xs = p + (prev - p) * mu_shift
    r, k, v, wd, ad, gd = jnp.split(xs, RWKV_SPLITS, axis=-1)
    log_w = -jax.nn.softplus(-(w0 + jnp.tanh(wd) @ w_lora_up).astype(f32)) - 0.5
    decay = jnp.exp(-jnp.exp(log_w))
    a = jax.nn.sigmoid((a0 + ad @ a_lora_up).astype(f32))
    g = (jax.nn.sigmoid(gd) @ g_lora_up).astype(f32)
    hs = (B, T, RWKV_HEADS, HEAD_DIM)
    r4 = r.astype(f32).reshape(hs)
    v4 = v.astype(f32).reshape(hs)
    w4 = decay.reshape(hs)
    a4 = a.reshape(hs)
    kk = (k * k_k).astype(f32).reshape(hs)
    kk = kk * lax.rsqrt(jnp.maximum(jnp.sum(kk * kk, axis=-1, keepdims=True), 1e-24))
    k4 = k.astype(f32).reshape(hs) * (1.0 + (a4 - 1.0) * k_a.astype(f32).reshape(RWKV_HEADS, HEAD_DIM))
    y, s = _wkv_scan(r4, w4, k4, v4, kk, a4, wkv_prev.astype(f32))
    mean = jnp.mean(y, axis=-1, keepdims=True)
    var = jnp.mean(jnp.square(y - mean), axis=-1, keepdims=True)
    yn = ((y - mean) * lax.rsqrt(var + GN_EPS)).reshape(B, T, RWKV_WIDTH)
    yn = yn * ln_x_w.astype(f32) + ln_x_b.astype(f32)
    bonus = (jnp.sum(r4 * k4 * r_k.astype(f32), axis=-1, keepdims=True) * v4).reshape(B, T, RWKV_WIDTH)
    out = (yn + bonus) * g
    return out.astype(p.dtype), s


def _rel_bias(table, rel):
    idx = jnp.clip(rel, -REL_MAX, REL_MAX) + REL_MAX
    return table[:, idx].astype(jnp.float32)


def _attend(q, k, v, bias, valid):
    s = jnp.einsum('bhqd,bhkd->bhqk', q.astype(jnp.float32), k.astype(jnp.float32)) * (HEAD_DIM ** -0.5)
    s = s + bias[None]
    if valid is not None:
        s = jnp.where(valid, s, NEG_INF)
    p = jax.nn.softmax(s, axis=-1)
    return jnp.einsum('bhqk,bhkd->bhqd', p, v.astype(jnp.float32)).astype(q.dtype)


def _band_attention_prompt(q, k, v, rel_bias):
    B, H, T, Dh = q.shape
    n_chunks = T // CHUNK
    pad = ((0, 0), (0, 0), (ATT_WINDOW, 0), (0, 0))
    kp = jnp.pad(k, pad)
    vp = jnp.pad(v, pad)
    qi = jnp.arange(CHUNK)
    kj = jnp.arange(BAND)
    bias = _rel_bias(rel_bias, kj[None, :] - ATT_WINDOW - qi[:, None])

    def one_chunk(c):
        start = c * CHUNK
        qc = lax.dynamic_slice_in_dim(q, start, CHUNK, axis=2)
        kc = lax.dynamic_slice_in_dim(kp, start, BAND, axis=2)
        vc = lax.dynamic_slice_in_dim(vp, start, BAND, axis=2)
        valid = (start - ATT_WINDOW + kj)[None, :] >= 0
        return _attend(qc, kc, vc, bias, valid)

    o = lax.map(one_chunk, jnp.arange(n_chunks))
    return o.transpose(1, 0, 3, 2, 4).reshape(B, T, H * Dh)


def _band_attention_step(q, k, v, k_cache, v_cache, rel_bias):
    B, H, T, Dh = q.shape
    L = k_cache.shape[2]
    kf = jnp.concatenate([k_cache.astype(k.dtype), k], axis=2)
    vf = jnp.concatenate([v_cache.astype(v.dtype), v], axis=2)
    rel = (jnp.arange(L + T) - L)[None, :] - jnp.arange(T)[:, None]
    o = _attend(q, kf, vf, _rel_bias(rel_bias, rel), None)
    return o.transpose(0, 2, 1, 3).reshape(B, T, H * Dh)


def _layer(x, shift_prev, wkv_prev, k_cache, v_cache, w):
    (norm_ff1, w_ff1_gate, w_ff1_up, w_ff1_down, norm_mix, w_in, mu_shift, w0, w_lora_up,
     a0, a_lora_up, g_lora_up, k_k, k_a, r_k, ln_x_w, ln_x_b, rel_bias, w_out,
     norm_ff2, w_ff2_gate, w_ff2_up, w_ff2_down) = w
    B, T, _ = x.shape
    x = x + 0.5 * _swiglu(_rms_norm(x, norm_ff1), w_ff1_gate, w_ff1_up, w_ff1_down)
    h = _rms_norm(x, norm_mix)
    proj = h @ w_in
    p_rwkv = proj[..., :RWKV_COLS]
    p_att = proj[..., RWKV_COLS:]
    if shift_prev is None:
        shift_prev = jnp.zeros((B, RWKV_COLS), x.dtype)
        wkv_prev = jnp.zeros((B, RWKV_HEADS, HEAD_DIM, HEAD_DIM), jnp.float32)
    rwkv_out, wkv_new = _rwkv7(p_rwkv, shift_prev, wkv_prev, mu_shift, w0, w_lora_up, a0,
                               a_lora_up, g_lora_up, k_k, k_a, r_k, ln_x_w, ln_x_b)
    q, k, v = jnp.split(p_att, 3, axis=-1)
    q = q.reshape(B, T, ATT_HEADS, HEAD_DIM).transpose(0, 2, 1, 3)
    k = k.reshape(B, T, ATT_HEADS, HEAD_DIM).transpose(0, 2, 1, 3)
    v = v.reshape(B, T, ATT_HEADS, HEAD_DIM).transpose(0, 2, 1, 3)
    if k_cache is None:
        att = _band_attention_prompt(q, k, v, rel_bias)
        n_keep = min(ATT_WINDOW, T)
        k_rows, v_rows = k[:, :, T - n_keep:], v[:, :, T - n_keep:]
    else:
        att = _band_attention_step(q, k, v, k_cache, v_cache, rel_bias)
        k_rows, v_rows = k, v
    x = x + jnp.concatenate([rwkv_out, att], axis=-1) @ w_out
    x = x + 0.5 * _swiglu(_rms_norm(x, norm_ff2), w_ff2_gate, w_ff2_up, w_ff2_down)
    return x, p_rwkv[:, -1], wkv_new.astype(x.dtype), k_rows, v_rows


def setup_inputs(seed: int = 0) -> dict:
    key = jax.random.key(seed)
    ks = jax.random.split(key, 32)
    f32 = jnp.float32

    def nrm(k, shape, s):
        return jax.random.normal(k, shape, f32) * s

    L_cache = min(ATT_WINDOW, PAST_LEN)
    Dp = (DEPTH,)
    return {
        "x_prompt": nrm(ks[0], (BATCH, SEQ, D_MODEL), 1.0),
        "x_sample": nrm(ks[1], (DEC_BATCH, DEC_SEQ, D_MODEL), 1.0),
        "state_shift": nrm(ks[2], Dp + (DEC_BATCH, RWKV_COLS), 1.0),
        "state_wkv": nrm(ks[3], Dp + (DEC_BATCH, RWKV_HEADS, HEAD_DIM, HEAD_DIM), 0.3),
        "cache_attn_k": nrm(ks[4], Dp + (DEC_BATCH, ATT_HEADS, L_cache, HEAD_DIM), 1.0),
        "cache_attn_v": nrm(ks[5], Dp + (DEC_BATCH, ATT_HEADS, L_cache, HEAD_DIM), 1.0),
        "norm_ff1": 1.0 + nrm(ks[6], Dp + (D_MODEL,), 0.01),
        "w_ff1_gate": nrm(ks[7], Dp + (D_MODEL, D_FF), D_MODEL ** -0.5),
        "w_ff1_up": nrm(ks[8], Dp + (D_MODEL, D_FF), D_MODEL ** -0.5),
        "w_ff1_down": nrm(ks[9], Dp + (D_FF, D_MODEL), D_FF ** -0.5),
        "norm_mix": 1.0 + nrm(ks[10], Dp + (D_MODEL,), 0.01),
        "w_in": nrm(ks[11], Dp + (D_MODEL, IN_COLS), D_MODEL ** -0.5),
        "mu_shift": jax.random.uniform(ks[12], Dp + (RWKV_COLS,), f32),
        "w0": nrm(ks[13], Dp + (RWKV_WIDTH,), 0.5) - 0.5,
        "w_lora_up": nrm(ks[14], Dp + (DECAY_LORA, RWKV_WIDTH), 0.5 * DECAY_LORA ** -0.5),
        "a0": nrm(ks[15], Dp + (RWKV_WIDTH,), 0.1),
        "a_lora_up": nrm(ks[16], Dp + (ICLR_LORA, RWKV_WIDTH), 0.5 * ICLR_LORA ** -0.5),
        "g_lora_up": nrm(ks[17], Dp + (GATE_LORA, RWKV_WIDTH), GATE_LORA ** -0.5),
        "k_k": 0.85 + nrm(ks[18], Dp + (RWKV_WIDTH,), 0.02),
        "k_a": 1.0 + nrm(ks[19], Dp + (RWKV_WIDTH,), 0.02),
        "r_k": nrm(ks[20], Dp + (RWKV_HEADS, HEAD_DIM), 0.1),
        "ln_x_w": 1.0 + nrm(ks[21], Dp + (RWKV_WIDTH,), 0.01),
        "ln_x_b": nrm(ks[22], Dp + (RWKV_WIDTH,), 0.01),
        "rel_bias": nrm(ks[23], Dp + (ATT_HEADS, N_REL), 0.1),
        "w_out": nrm(ks[24], Dp + (MIX_WIDTH, D_MODEL), MIX_WIDTH ** -0.5),
        "norm_ff2": 1.0 + nrm(ks[25], Dp + (D_MODEL,), 0.01),
        "w_ff2_gate": nrm(ks[26], Dp + (D_MODEL, D_FF), D_MODEL ** -0.5),
        "w_ff2_up": nrm(ks[27], Dp + (D_MODEL, D_FF), D_MODEL ** -0.5),
        "w_ff2_down": nrm(ks[28], Dp + (D_FF, D_MODEL), D_FF ** -0.5),
        "norm_final": 1.0 + nrm(ks[29], (D_MODEL,), 0.01),
    }


def reference(x_prompt, x_sample, state_shift, state_wkv, cache_attn_k, cache_attn_v,
              norm_ff1, w_ff1_gate, w_ff1_up, w_ff1_down, norm_mix, w_in, mu_shift, w0,
              w_lora_up, a0, a_lora_up, g_lora_up, k_k, k_a, r_k, ln_x_w, ln_x_b,
              rel_bias, w_out, norm_ff2, w_ff2_gate, w_ff2_up, w_ff2_down, norm_final):
    xp = x_prompt
    xs = x_sample
    p_shift, p_wkv, p_k, p_v = [], [], [], []
    s_shift, s_wkv, s_k, s_v = [], [], [], []
    for l in range(DEPTH):
        w = (norm_ff1[l], w_ff1_gate[l], w_ff1_up[l], w_ff1_down[l], norm_mix[l], w_in[l],
             mu_shift[l], w0[l], w_lora_up[l], a0[l], a_lora_up[l], g_lora_up[l], k_k[l],
             k_a[l], r_k[l], ln_x_w[l], ln_x_b[l], rel_bias[l], w_out[l], norm_ff2[l],
             w_ff2_gate[l], w_ff2_up[l], w_ff2_down[l])
        xp, sh, wk, kr, vr = _layer(xp, None, None, None, None, w)
        p_shift.append(sh); p_wkv.append(wk); p_k.append(kr); p_v.append(vr)
        xs, sh, wk, kr, vr = _layer(xs, state_shift[l], state_wkv[l], cache_attn_k[l],
                                    cache_attn_v[l], w)
        s_shift.append(sh); s_wkv.append(wk); s_k.append(kr); s_v.append(vr)
    y_prompt = _rms_norm(xp, norm_final)
    y_sample = _rms_norm(xs, norm_final)
    return (y_prompt, y_sample,
            jnp.stack(p_shift), jnp.stack(p_wkv), jnp.stack(p_k), jnp.stack(p_v),
            jnp.stack(s_shift), jnp.stack(s_wkv), jnp.stack(s_k), jnp.stack(s_v))
```

```python
import contextlib
import math
import numpy as np
import concourse.bass as bass
import concourse.mybir as mybir
from concourse.bass_utils import run_bass_kernel_spmd

F32 = mybir.dt.float32
BF16 = mybir.dt.bfloat16
AF = mybir.ActivationFunctionType
ALU = mybir.AluOpType
AX = mybir.AxisListType

PE, ACT, DVE, POOL, SP = range(5)
NE = 5

D = 1024
KC = 8
DFF = 2816
FC = 22
RW = 1792
INC = 3328
NH = 8
HD = 64
TT = 512
C0 = math.exp(-0.5)
NORM_EPS = 1e-5
GN_EPS = 64e-5


class Buf:
    __slots__ = ("name", "w", "re", "rd")

    def __init__(self, name):
        self.name = name
        self.w = None
        self.re = {}
        self.rd = []


class Eng:
    def __init__(self, idx, eng, sem):
        self.idx, self.eng, self.sem = idx, eng, sem
        self.n = 0
        self.known = [0] * NE
        self.snaps = []
        self.dknown = {}
        self.nwaits = 0
        self.self_waited = 0


class FW:
    def __init__(self, nc, stack, n_dma_sems=40, same_engine_sync=False):
        self.nc = nc
        self.same_engine_sync = same_engine_sync
        engs = [nc.tensor, nc.scalar, nc.vector, nc.gpsimd, nc.sync]
        names = ["pe", "act", "dve", "pool", "sp"]
        self.engs = []
        for i, (e, n) in enumerate(zip(engs, names)):
            sem = stack.enter_context(nc.semaphore("sem_" + n))
            self.engs.append(Eng(i, e, sem))
        n_sw = 8
        self.dma_sems = [stack.enter_context(nc.semaphore(f"sem_dma{i}")) for i in range(n_dma_sems + n_sw)]
        self.dma_sem_val = [0] * (n_dma_sems + n_sw)
        self.dma_rr = 0
        self.n_hw = n_dma_sems
        self.n_sw = n_sw
        self.dma_rr_sw = 0
        self.out_deps = []
        self.n_ops = [0] * NE
        self.n_dma = 0

    def _wait(self, e, dep):
        if dep is None:
            return
        if dep[0] == 'e':
            _, j, k = dep
            if j == e.idx:
                if not self.same_engine_sync or j == PE or j == SP:
                    return
                if e.self_waited >= k:
                    return
                e.eng.wait_ge(e.sem, k)
                e.nwaits += 1
                e.self_waited = k
                return
            if e.known[j] >= k:
                return
            e.eng.wait_ge(self.engs[j].sem, k)
            e.nwaits += 1
            snap = self.engs[j].snaps[k - 1]
            e.known = [max(a, b) for a, b in zip(e.known, snap)]
            if e.known[j] < k:
                e.known[j] = k
        else:
            _, si, val = dep
            if e.dknown.get(si, 0) >= val:
                return
            e.eng.wait_ge(self.dma_sems[si], val)
            e.nwaits += 1
            e.dknown[si] = val

    def _pre(self, e, reads, writes):
        for b in reads:
            self._wait(e, b.w)
        for b in writes:
            self._wait(e, b.w)
            for j, k in b.re.items():
                self._wait(e, ('e', j, k))
            for d in b.rd:
                self._wait(e, d)

    def op(self, ei, fn, reads=(), writes=()):
        e = self.engs[ei]
        self._pre(e, reads, writes)
        ins = fn(e.eng)
        e.n += 1
        ins.then_inc(e.sem, 1)
        e.known[e.idx] = e.n
        e.snaps.append(tuple(e.known))
        dep = ('e', e.idx, e.n)
        for b in reads:
            if b.re.get(e.idx, 0) < e.n:
                b.re[e.idx] = e.n
        for b in writes:
            b.w = dep
            b.re = {}
            b.rd = []
        self.n_ops[ei] += 1
        return dep

    def dma(self, qi, out, in_, reads=(), writes=(), is_output=False, **kw):
        e = self.engs[qi]
        self._pre(e, reads, writes)
        if qi == POOL:
            si = self.n_hw + self.dma_rr_sw
            self.dma_rr_sw = (self.dma_rr_sw + 1) % self.n_sw
        else:
            si = self.dma_rr
            self.dma_rr = (self.dma_rr + 1) % self.n_hw
        if self.dma_sem_val[si] > 0:
            self._wait(e, ('d', si, self.dma_sem_val[si]))
        ins = e.eng.dma_start(out=out, in_=in_, **kw)
        self.dma_sem_val[si] += 16
        ins.then_inc(self.dma_sems[si], 16)
        dep = ('d', si, self.dma_sem_val[si])
        for b in reads:
            b.rd.append(dep)
        for b in writes:
            b.w = dep
            b.re = {}
            b.rd = []
        if is_output:
            self.out_deps.append(dep)
        self.n_dma += 1
        return dep

    def finish(self):
        e = self.engs[SP]
        for d in self.out_deps:
            self._wait(e, d)
        for si, v in enumerate(self.dma_sem_val):
            if v > 0:
                self._wait(e, ('d', si, v))
        for j in range(NE):
            if j != SP and self.engs[j].n > 0:
                self._wait(e, ('e', j, self.engs[j].n))


class Bank:
    def __init__(self, pool, i):
        self.pool, self.i = pool, i
        self.t, self.buf = pool.banks[i]

    def f32(self):
        return self.t[:]

    def bf16(self):
        return self.t[:].bitcast(BF16)

    def put(self):
        self.pool.open.remove(self.i)
        self.pool.free.append(self.i)


class PsumPool:
    def __init__(self, nc, stack):
        self.banks = []
        for i in range(8):
            t = stack.enter_context(nc.psum_tensor(f"psb{i}", [128, 512], F32))
            self.banks.append((t, Buf(f"psb{i}")))
        self.free = list(range(8))
        self.open = set()

    GROUPS = {"att": (0, 1, 2, 3), "prep": (4, 5), "chunk": (6, 7)}

    def get(self, grp=None):
        cand = self.free if grp is None else [i for i in self.free if i in self.GROUPS[grp]]
        assert cand, f"no free PSUM bank for {grp}: open={sorted(self.open)}"
        i = cand[0]
        self.free.remove(i)
        self.open.add(i)
        return Bank(self, i)


class TileT:
    def __init__(self, t, b):
        self.t, self.b = t, b

    def __getitem__(self, k):
        return self.t[k]


WEIGHT_NAMES = ["norm_ff1", "w_ff1_gate", "w_ff1_up", "w_ff1_down", "norm_mix", "w_in", "mu_shift", "w0",
                "w_lora_up", "a0", "a_lora_up", "g_lora_up", "k_k", "k_a", "r_k", "ln_x_w", "ln_x_b",
                "rel_bias", "w_out", "norm_ff2", "w_ff2_gate", "w_ff2_up", "w_ff2_down", "norm_final"]
WEIGHT_SHAPES = {
    "norm_ff1": [D], "w_ff1_gate": [D, DFF], "w_ff1_up": [D, DFF], "w_ff1_down": [DFF, D], "norm_mix": [D],
    "w_in": [D, INC], "mu_shift": [RW], "w0": [512], "w_lora_up": [64, 512], "a0": [512], "a_lora_up": [64, 512],
    "g_lora_up": [128, 512], "k_k": [512], "k_a": [512], "r_k": [512], "ln_x_w": [512], "ln_x_b": [512],
    "rel_bias": [8, 257], "w_out": [D, D], "norm_ff2": [D], "w_ff2_gate": [D, DFF], "w_ff2_up": [D, DFF],
    "w_ff2_down": [DFF, D], "norm_final": [D],
}


def build_program(T, nsamp=2, dbg=None, same_engine_sync=True, stop_after=None):
    dbg = dbg or set()
    LVL = {"ffn1": 1, "proj": 2, "prep": 3, "chunk": 4, "rwkvout": 5, "attn": 6, "mixT": 7, None: 9}[stop_after]
    nc = bass.Bass("TRN2", target_bir_lowering=False)
    n_ptiles = T // TT
    dram = {}

    def din(name, shape, dt=F32):
        dram[name] = nc.dram_tensor(name, shape, dt, kind="ExternalInput")
        return dram[name]

    def dout(name, shape, dt=F32):
        dram[name] = nc.dram_tensor(name, shape, dt, kind="ExternalOutput")
        return dram[name]

    def dint(name, shape, dt):
        dram[name] = nc.dram_tensor(name, shape, dt, kind="Internal")
        return dram[name]

    x_d = din("x", [T, D])
    xs_d = din("xs", [2, 16, D])
    stsh_d = din("st_shift", [2, RW])
    stwkv_d = din("st_wkv", [2, NH, HD, HD])
    ck_d = din("ck", [2, NH, 512, HD])
    cv_d = din("cv", [2, NH, 512, HD])
    W = {n: din(n, WEIGHT_SHAPES[n]) for n in WEIGHT_NAMES}

    y_d = dout("y", [T, D])
    ys_d = dout("ys", [2, 16, D])
    pshift_d = dout("p_shift", [1, RW])
    pwkv_d = dout("p_wkv", [NH, HD, HD])
    pk_d = dout("p_k", [NH, 512, HD])
    pv_d = dout("p_v", [NH, 512, HD])
    sshift_d = dout("s_shift", [2, RW])
    swkv_d = dout("s_wkv", [2, NH, HD, HD])
    sk_d = dout("s_k", [2, NH, 16, HD])
    sv_d = dout("s_v", [2, NH, 16, HD])

    big_w = ["w_ff1_gate", "w_ff1_up", "w_ff1_down", "w_in", "w_out", "w_ff2_gate", "w_ff2_up", "w_ff2_down"]
    WB = {n: dint(n + "_bf", WEIGHT_SHAPES[n], BF16) for n in big_w}
    WBb = {n: [Buf(f"{n}_bf{r}") for r in range(WEIGHT_SHAPES[n][0] // 128)] for n in big_w}
    tabpad_d = dint("tabpad", [8, 512], F32)
    TABPAD = Buf("tabpad")
    tabrev_d = dint("tabrev", [8, 512], F32)
    TABREV = Buf("tabrev")
    carry_d = dint("carry", [1, RW], F32)
    CARRY = Buf("carry")

    dbg_outs = {}

    with contextlib.ExitStack() as stack:
        fw = FW(nc, stack, same_engine_sync=same_engine_sync)
        pp = PsumPool(nc, stack)
        sb_bytes = [0]

        def sb(name, shape, dt=F32):
            t = stack.enter_context(nc.sbuf_tensor("s_" + name, shape, dt))
            n = 1
            for s in shape[1:]:
                n *= s
            sb_bytes[0] += n * (4 if dt == F32 else 2)
            return TileT(t, Buf(name))

        NSLOT = 3
        ring = [sb(f"ring{i}", [128, 4096], BF16) for i in range(NSLOT)]
        xT = sb("xT", [128, KC, TT])
        hT = sb("hT", [128, KC, TT], BF16)
        BIG = sb("BIG", [128, 7168])
        BIGf = BIG.t[:]
        BIGb = BIG.t[:].bitcast(BF16)
        act_v = BIGb[:, 0:FC * TT].rearrange("p (f n) -> p f n", n=TT)
        sq_v = BIGb[:, 0:KC * TT].rearrange("p (k n) -> p k n", n=TT)
        pall_v = BIGf[:, 0:4 * RW].rearrange("p (g c) -> p g c", c=RW)
        xin_v = BIGf[:, 0:4 * D].rearrange("p (g c) -> p g c", c=D)
        yst_v = BIGf[:, 4096:4096 + 2 * D].rearrange("p (g c) -> p g c", c=D)
        xs_t = sb("xs", [128, RW])
        XS0 = Buf("xs_row0")
        XS1 = Buf("xs_tail")
        NA = 9
        A = [sb(f"A{i}", [128, 512]) for i in range(NA)]
        rstd = A[0]
        silu = A[1]
        TMV_ = sb("TMV", [128, 512])
        TMV = TileT(TMV_.t[:].rearrange("p (h c) -> p h c", c=64), TMV_.b)
        ysb2 = [sb(f"ysb{i}", [128, 512]) for i in range(2)]
        ysb = ysb2[0]
        ysq = TMV_
        bonus2 = [sb(f"bonus{i}", [128, 512]) for i in range(2)]
        gsb2 = [sb(f"gsb{i}", [128, 512]) for i in range(2)]
        bonus, gsb = bonus2[0], gsb2[0]
        small_p = sb("small_p", [128, 24])
        small_o = sb("small_o", [128, 24])
        small_a = sb("small_a", [128, 8])
        small_f = sb("small_f", [128, 4])
        lin = sb("lin", [128, 256], BF16)
        linT = sb("linT", [128, 2, 128], BF16)
        TMA = sb("TMA", [128, 7, 512], BF16)
        TMAb = [Buf(f"tma{i}") for i in range(7)]
        T1 = sb("T1", [64, NH, 2, 2, 64], BF16)
        T2 = sb("T2", [64, NH, 2, 2, 64], BF16)
        wc2 = sb("wc", [64, 2, NH, 2])
        vaug = [sb(f"vaug{i}", [128, NH, 65], BF16) for i in range(8)]
        QT = sb("QT", [128, 4, TT], BF16)
        KT = [sb(f"KT{i}", [128, 4, 128], BF16) for i in range(8)]
        cm = {n: sb("c_" + n, [128, NH, 64], BF16) for n in
              ["A", "AT", "RBT", "RKT", "MT", "X0", "X1", "XT0", "XT1", "P0", "P1", "MV", "U"]}
        TKT2 = sb("TKT2", [64, 2, NH, 64], BF16)
        S32 = sb("S32", [64, NH, 64])
        Sbf = sb("Sbf", [64, NH, 64], BF16)
        Stmp = sb("Stmp", [64, NH, 64])
        mix2 = [sb(f"mix{i}", [128, 1024], BF16) for i in range(2)]
        PT = [sb(f"PT{i}", [128, 5, 128], BF16) for i in range(1)]
        mu_b = sb("mu_b", [128, RW])
        cb512 = {n: sb("b_" + n, [128, 512]) for n in ["w0", "a0", "k_k", "k_a", "r_k", "ln_x_w", "ln_x_b"]}
        wal = sb("wal", [128, 512], BF16)
        gl = sb("gl", [128, 512], BF16)
        T3b = sb("T3b", [128, NH, 128], BF16)
        T4b = sb("T4b", [128, NH, 128], BF16)
        idf = sb("idf", [128, 128])
        idb = sb("idb", [128, 128], BF16)
        ones_bf = sb("ones_bf", [128, 128], BF16)
        tri = sb("tri", [128, 128])
        bo = sb("bo", [128, 128])
        ci = sb("ci", [128, 2])
        msk = {n: sb("m_" + n, [128, 64]) for n in ["NSU", "SU", "UI", "NSL", "IDR"]}
        cbias = sb("cbias", [128, NH])
        gains = sb("gains", [128, 3, KC])

        def tt_(ei, out, in0, in1, op, reads, writes):
            return fw.op(ei, lambda e: e.tensor_tensor(out=out, in0=in0, in1=in1, op=op), reads, writes)

        def ts_(ei, out, in0, s1, op0, reads, writes, s2=None, op1=None):
            if op1 is None:
                return fw.op(ei, lambda e: e.tensor_scalar(out=out, in0=in0, scalar1=s1, scalar2=None, op0=op0), reads, writes)
            return fw.op(ei, lambda e: e.tensor_scalar(out=out, in0=in0, scalar1=s1, scalar2=s2, op0=op0, op1=op1), reads, writes)

        def stt_(out, in0, scalar, in1, op0, op1, reads, writes):
            return fw.op(DVE, lambda e: e.scalar_tensor_tensor(out=out, in0=in0, scalar=scalar, in1=in1, op0=op0, op1=op1), reads, writes)

        def act_(out, in_, func, reads, writes, **kw):
            return fw.op(ACT, lambda e: e.activation(out=out, in_=in_, func=func, **kw), reads, writes)

        def copy_(ei, out, in_, reads, writes):
            if ei == ACT:
                return act_(out, in_, AF.Copy, reads, writes)
            return fw.op(ei, lambda e: e.tensor_copy(out=out, in_=in_), reads, writes)

        def mm_(out, lhsT, rhs, reads, writes, start=True, stop=True, **kw):
            return fw.op(PE, lambda e: e.matmul(out, lhsT, rhs, start=start, stop=stop, **kw), reads, writes)

        def tr_(out, in_, ident, reads, writes):
            return fw.op(PE, lambda e: e.transpose(out, in_, ident), reads, writes)

        def dbg_dump(name, ap, shape, reads):
            if name not in dbg:
                return
            key = name
            i = 0
            while key in dbg_outs:
                i += 1
                key = f"{name}_{i}"
            o = dout("dbg_" + key, list(shape))
            dbg_outs[key] = o
            fw.dma(SP, o.ap(), ap, reads=reads, is_output=True)

        def bcast_row(dst, src_ap_tensor, off, n, eng=SP):
            fw.dma(eng, dst.t[:, 0:n], bass.AP(src_ap_tensor, off, [[0, 128], [1, n]]), writes=[dst.b])

        bcast_row(mu_b, W["mu_shift"].ap().tensor, 0, RW)
        for n in cb512:
            bcast_row(cb512[n], W[n].ap().tensor, 0, 512)
        GT, R8, TP8 = Buf("gT"), Buf("r8"), Buf("tp8")
        gT_v = BIGf[0:8, 3072:3456].rearrange("p (i c) -> p i c", c=128)
        r8_v = BIGf[0:8, 3456:3457]
        D8_v = BIGf[0:8, 3460:3468]
        one8_v = BIGf[0:8, 3472:3600]
        tp8_v = BIGf[0:8, 3600:4112]
        with nc.allow_non_contiguous_dma(reason="small constant loads"):
            for i, n in enumerate(["norm_ff1", "norm_mix", "norm_ff2"]):
                fw.dma(SP, gT_v[:, i, :], W[n].ap().rearrange("(k p) -> k p", p=128), writes=[GT])
            fw.dma(SP, r8_v, bass.AP(W["rel_bias"].ap().tensor, 0, [[257, NH], [1, 1]]), writes=[R8])
            fw.dma(SP, tp8_v[:, 128:385], W["rel_bias"].ap(), writes=[TP8])
        fw.op(POOL, lambda e: e.memset(idf.t[:], 0.0), writes=[idf.b])
        fw.op(POOL, lambda e: e.affine_select(out=idf.t[:], in_=idf.t[:], pattern=[[-1, 128]], compare_op=ALU.not_equal,
                                              fill=1.0, base=0, channel_multiplier=1), writes=[idf.b])
        copy_(POOL, idb.t[:], idf.t[:], [idf.b], [idb.b])
        fw.op(POOL, lambda e: e.memset(ones_bf.t[:], 1.0), writes=[ones_bf.b])
        fw.op(POOL, lambda e: e.memset(tri.t[:], 1.0), writes=[tri.b])
        fw.op(POOL, lambda e: e.affine_select(out=tri.t[:], in_=tri.t[:], pattern=[[1, 128]], compare_op=ALU.is_ge,
                                              fill=0.0, base=0, channel_multiplier=-1), writes=[tri.b])
        fw.op(POOL, lambda e: e.memset(tri.t[0:64, 64:128], 0.0), writes=[tri.b])
        fw.op(POOL, lambda e: e.memset(bo.t[:], 0.0), writes=[bo.b])
        fw.op(POOL, lambda e: e.memset(ci.t[:], 0.0), writes=[ci.b])
        for blk in range(2):
            ps = slice(blk * 64, blk * 64 + 64)
            fw.op(POOL, lambda e: e.memset(bo.t[ps, ps], 1.0), writes=[bo.b])
            fw.op(POOL, lambda e: e.memset(ci.t[ps, blk:blk + 1], 1.0), writes=[ci.b])
        mfull = BIGf[:, 0:128]
        for n, val, step, cmul, op_, fill in [("NSU", -1.0, 1, -1, ALU.is_gt, 0.0), ("SU", 1.0, 1, -1, ALU.is_gt, 0.0),
                                             ("UI", 1.0, 1, -1, ALU.is_ge, 0.0), ("NSL", -1.0, -1, 1, ALU.is_gt, 0.0),
                                             ("IDR", 0.0, 1, -1, ALU.not_equal, 1.0)]:
            m = msk[n]
            fw.op(POOL, lambda e: e.memset(mfull, val), writes=[BIG.b])
            fw.op(POOL, lambda e: e.affine_select(out=mfull, in_=mfull, pattern=[[step, 128]], compare_op=op_,
                                                  fill=fill, base=0, channel_multiplier=cmul), writes=[BIG.b])
            copy_(POOL, m.t[0:64, :], BIGf[0:64, 0:64], [BIG.b], [m.b])
            copy_(POOL, m.t[64:128, :], BIGf[64:128, 64:128], [BIG.b], [m.b])
        bk = pp.get()
        for i in range(3):
            tr_(bk.f32()[:, i * 8:(i + 1) * 8], gT_v[:, i, :], idf.t[0:8, 0:8], [GT, idf.b], [bk.buf])
        copy_(DVE, gains.t[:, :, :], bk.f32()[:, 0:24].rearrange("p (i k) -> p i k", k=8), [], [bk.buf, gains.b])
        bk.put()
        ts_(DVE, D8_v, idf.t[0:8, 0:8], r8_v, ALU.mult, [idf.b, R8], [GT])
        fw.op(DVE, lambda e: e.memset(one8_v, 1.0), writes=[GT])
        bk = pp.get()
        mm_(bk.f32()[:, 0:8], one8_v, D8_v, [GT], [bk.buf])
        copy_(DVE, cbias.t[:, :], bk.f32()[:, 0:8], [], [bk.buf, cbias.b])
        bk.put()
        fw.op(DVE, lambda e: e.memset(tp8_v[:, 0:128], 0.0), writes=[TP8])
        ts_(DVE, tp8_v[:, 0:128], tp8_v[:, 0:128], r8_v, ALU.add, [R8], [TP8])
        fw.op(DVE, lambda e: e.memset(tp8_v[:, 385:512], 0.0), reads=[R8], writes=[TP8])
        fw.dma(SP, tabpad_d.ap(), tp8_v, reads=[TP8], writes=[TABPAD])
        fw.dma(SP, BIGf[0:64, 0:512], W["w_lora_up"].ap(), writes=[BIG.b])
        fw.dma(SP, BIGf[64:128, 0:512], W["a_lora_up"].ap(), writes=[BIG.b])
        fw.dma(SP, BIGf[:, 512:1024], W["g_lora_up"].ap(), writes=[BIG.b])
        copy_(DVE, wal.t[:], BIGf[:, 0:512], [BIG.b], [wal.b])
        copy_(DVE, gl.t[:], BIGf[:, 512:1024], [BIG.b], [gl.b])
        with nc.allow_non_contiguous_dma(reason="bias table"):
            tpt = tabpad_d.ap().tensor
            TOE = [Buf(f"toe{i}") for i in range(16)]
            for h in range(NH):
                fw.dma(SP, BIGf[:, 1024 + h * 128:1024 + (h + 1) * 128], bass.AP(tpt, 512 * h + 1, [[1, 128], [1, 128]]),
                       reads=[TABPAD], writes=[TOE[2 * h]])
                fw.dma(SP, BIGf[:, 2048 + h * 128:2048 + (h + 1) * 128], bass.AP(tpt, 512 * h + 129, [[1, 128], [1, 128]]),
                       reads=[TABPAD], writes=[TOE[2 * h + 1]])
        Jm = A[0]
        fw.op(POOL, lambda e: e.memset(Jm.t[:, 0:128], 0.0), writes=[Jm.b])
        fw.op(POOL, lambda e: e.affine_select(out=Jm.t[:, 0:128], in_=Jm.t[:, 0:128], pattern=[[1, 128]], compare_op=ALU.not_equal,
                                              fill=1.0, base=-127, channel_multiplier=1), writes=[Jm.b])
        for ti_, (tdst, cbase) in enumerate([(T3b, 1024), (T4b, 2048)]):
            for half in range(2):
                bk = pp.get()
                for hh in range(4):
                    h = half * 4 + hh
                    mm_(bk.f32()[:, hh * 128:(hh + 1) * 128], BIGf[:, cbase + h * 128:cbase + (h + 1) * 128], Jm.t[:, 0:128],
                        [TOE[2 * h + ti_], Jm.b], [bk.buf])
                ts_(DVE, tdst.t[:, half * 4:half * 4 + 4, :].rearrange("p h q -> p (h q)"), bk.f32()[:, :], 8.0, ALU.mult, [], [bk.buf, tdst.b])
                bk.put()
        fw.op(DVE, lambda e: e.memset(BIGf[:, 1024:1025], 0.0), writes=[BIG.b, GT, R8, TP8] + TOE)
        for i in range(8):
            fw.op(POOL, lambda e: e.memset(vaug[i].t[:, :, 64:65], 1.0), writes=[vaug[i].b])
        for i in range(1):
            fw.op(POOL, lambda e: e.memset(PT[i].t[:], 0.0), writes=[PT[i].b])

        def convert_weights(names, after=()):
            for n in names:
                rows = WEIGHT_SHAPES[n][0]
                for r in range(rows // 128):
                    fw.dma(POOL, WB[n].ap()[r * 128:(r + 1) * 128, :], W[n].ap()[r * 128:(r + 1) * 128, :], reads=list(after), writes=[WBb[n][r]])

        convert_weights(["w_ff1_gate", "w_ff1_up", "w_ff1_down", "w_in"])

        def slab_list_for_tile(ntok, emit_kv):
            L = []
            for ffn_i, (gn, un, dn) in enumerate([("w_ff1_gate", "w_ff1_up", "w_ff1_down")]):
                pass
            def ffn_slabs(gn, un, dn):
                for s in range(11):
                    L.append(("gu", [(0, WB[gn], WBb[gn], 256 * s, 256, 0, 8), (2048, WB[un], WBb[un], 256 * s, 256, 0, 8)]))
                for dp in range(4):
                    for fh in range(2):
                        L.append(("dn", [(0, WB[dn], WBb[dn], 256 * dp, 256, fh * 11, 11)]))
            ffn_slabs("w_ff1_gate", "w_ff1_up", "w_ff1_down")
            if LVL <= 1:
                return L
            for c0 in [0, 512, 1024, 1536]:
                w = min(512, RW - c0)
                L.append(("in", [(0, WB["w_in"], WBb["w_in"], c0, w, 0, 8)]))
            for c0 in [RW, RW + 512, RW + 1024]:
                L.append(("in", [(0, WB["w_in"], WBb["w_in"], c0, 512, 0, 8)]))
            if LVL <= 7:
                return L
            for c0 in [0, 512]:
                L.append(("out", [(0, WB["w_out"], WBb["w_out"], c0, 512, 0, 8)]))
            ffn_slabs("w_ff2_gate", "w_ff2_up", "w_ff2_down")
            return L

        class Slabs:
            def __init__(self):
                self.list = []
                self.issued = 0
                self.cur = 0

            def issue_upto(self, k):
                while self.issued <= k and self.issued < len(self.list):
                    j = self.issued
                    slot = ring[j % NSLOT]
                    for (off, wt, wbufs, c0, w, r0, nr) in self.list[j][1]:
                        dst = slot.t[:, off:off + nr * w].rearrange("p (k c) -> p k c", c=w)
                        src = wt.ap()[r0 * 128:(r0 + nr) * 128, c0:c0 + w].rearrange("(k p) c -> p k c", p=128)
                        fw.dma(SP, dst, src, reads=wbufs[r0:r0 + nr], writes=[slot.b])
                    self.issued += 1

            def acquire(self, kind):
                k = self.cur
                assert self.list[k][0] == kind, (self.list[k][0], kind)
                self.issue_upto(k + NSLOT - 1)
                return ring[k % NSLOT]

            def release(self):
                self.cur += 1

        slabs = Slabs()

        tiles = []
        for i in range(n_ptiles):
            tiles.append(dict(kind="p", idx=i, t0=i * TT, ntok=TT, groups=[(g * 128, 128) for g in range(4)],
                              first=(i == 0), last=(i == n_ptiles - 1), seq=0))
        for s in range(nsamp):
            tiles.append(dict(kind="s", idx=0, t0=0, ntok=16, groups=[(0, 16)], first=True, last=True, seq=s))
        for tl in tiles:
            slabs.list += slab_list_for_tile(tl["ntok"], tl["last"])

        class NormAcc:
            def __init__(self, ntok):
                self.ntok = ntok
                self.bank = pp.get()
                self.pending = []
                self.n = 0

            def flush(self):
                for k in self.pending:
                    mm_(self.bank.f32()[:, 0:self.ntok], ones_bf.t[:], QT.t[:, k % 4, 0:self.ntok], [ones_bf.b, QT.b], [self.bank.buf],
                        start=(self.n == 0), stop=(self.n == KC - 1))
                    self.n += 1
                self.pending = []

            def add(self, k):
                self.flush()
                act_(QT.t[:, k % 4, 0:self.ntok], xT.t[:, k, 0:self.ntok], AF.Square, [xT.b], [QT.b])
                self.pending.append(k)

            def finish(self, gi):
                self.flush()
                assert self.n == KC
                ntok = self.ntok
                bk = self.bank
                act_(rstd.t[:, 0:ntok], bk.f32()[:, 0:ntok], AF.Sqrt, [], [bk.buf, rstd.b], scale=1.0 / D, bias=NORM_EPS)
                bk.put()
                fw.op(DVE, lambda e: e.reciprocal(out=rstd.t[:, 0:ntok], in_=rstd.t[:, 0:ntok]), [], [rstd.b])
                for k in range(KC):
                    stt_(hT.t[:, k, 0:ntok], xT.t[:, k, 0:ntok], gains.t[:, gi, k:k + 1], rstd.t[:, 0:ntok], ALU.mult, ALU.mult,
                         [xT.b, gains.b, rstd.b], [hT.b])

        def rms_feat(gi, ntok):
            act_(sq_v[:, :, 0:ntok], xT.t[:, :, 0:ntok], AF.Square, [xT.b], [BIG.b])
            bk = pp.get()
            for k in range(KC):
                mm_(bk.f32()[:, 0:ntok], ones_bf.t[:], sq_v[:, k, 0:ntok], [ones_bf.b, BIG.b], [bk.buf], start=(k == 0), stop=(k == KC - 1))
            act_(rstd.t[:, 0:ntok], bk.f32()[:, 0:ntok], AF.Sqrt, [], [bk.buf, rstd.b], scale=1.0 / D, bias=NORM_EPS)
            bk.put()
            fw.op(DVE, lambda e: e.reciprocal(out=rstd.t[:, 0:ntok], in_=rstd.t[:, 0:ntok]), [], [rstd.b])
            for k in range(KC):
                stt_(hT.t[:, k, 0:ntok], xT.t[:, k, 0:ntok], gains.t[:, gi, k:k + 1], rstd.t[:, 0:ntok], ALU.mult, ALU.mult,
                     [xT.b, gains.b, rstd.b], [hT.b])

        def ffn(gi, ntok, acc=None, want_acc=False):
            if acc is not None:
                acc.finish(gi)
            else:
                rms_feat(gi, ntok)
            nacc = None
            for s in range(11):
                slot = slabs.acquire("gu")
                wg = slot.t[:, 0:2048].rearrange("p (k c) -> p k c", c=256)
                wu = slot.t[:, 2048:4096].rearrange("p (k c) -> p k c", c=256)
                for j in range(2):
                    fc = 2 * s + j
                    bg = pp.get()
                    bu = pp.get()
                    for k in range(KC):
                        mm_(bg.f32()[:, 0:ntok], wg[:, k, j * 128:(j + 1) * 128], hT.t[:, k, 0:ntok], [slot.b, hT.b], [bg.buf],
                            start=(k == 0), stop=(k == KC - 1))
                    for k in range(KC):
                        mm_(bu.f32()[:, 0:ntok], wu[:, k, j * 128:(j + 1) * 128], hT.t[:, k, 0:ntok], [slot.b, hT.b], [bu.buf],
                            start=(k == 0), stop=(k == KC - 1))
                    act_(silu.t[:, 0:ntok], bg.f32()[:, 0:ntok], AF.Silu, [], [bg.buf, silu.b])
                    bg.put()
                    tt_(DVE, act_v[:, fc, 0:ntok], silu.t[:, 0:ntok], bu.f32()[:, 0:ntok], ALU.mult, [silu.b], [bu.buf, BIG.b])
                    bu.put()
                slabs.release()
            for dp in range(4):
                if want_acc and dp == 0:
                    nacc = NormAcc(ntok)
                bks = [pp.get(), pp.get()]
                for fh in range(2):
                    slot = slabs.acquire("dn")
                    wd = slot.t[:, 0:11 * 256].rearrange("p (k c) -> p k c", c=256)
                    for j in range(2):
                        for f in range(11):
                            mm_(bks[j].f32()[:, 0:ntok], wd[:, f, j * 128:(j + 1) * 128], act_v[:, fh * 11 + f, 0:ntok],
                                [slot.b, BIG.b], [bks[j].buf], start=(fh == 0 and f == 0), stop=(fh == 1 and f == 10))
                    slabs.release()
                for j in range(2):
                    dc = 2 * dp + j
                    stt_(xT.t[:, dc, 0:ntok], bks[j].f32()[:, 0:ntok], 0.5, xT.t[:, dc, 0:ntok], ALU.mult, ALU.add,
                         [], [bks[j].buf, xT.b])
                    bks[j].put()
                    if nacc is not None:
                        nacc.add(dc)
            return nacc

        XIN = [A[4], A[5], A[6], A[7], A[8], bonus, gsb, ysb]

        def issue_x_load(tl):
            for gi, (goff, gtok) in enumerate(tl["groups"]):
                for j in range(2):
                    xt_ = XIN[gi * 2 + j]
                    if tl["kind"] == "p":
                        src = x_d.ap()[tl["t0"] + goff:tl["t0"] + goff + gtok, j * 512:(j + 1) * 512]
                    else:
                        src = xs_d.ap()[tl["seq"], :, j * 512:(j + 1) * 512]
                    fw.dma(SP, xt_.t[0:gtok, :], src, writes=[xt_.b])

        def load_x(tl):
            ntok = tl["ntok"]
            acc = NormAcc(ntok)
            for k in range(KC):
                bk = pp.get()
                for gi, (goff, gtok) in enumerate(tl["groups"]):
                    xt_ = XIN[gi * 2 + k // 4]
                    tr_(bk.f32()[:, goff:goff + gtok], xt_.t[0:gtok, (k % 4) * 128:(k % 4 + 1) * 128], idf.t[0:gtok, 0:gtok],
                        [xt_.b, idf.b], [bk.buf])
                copy_(ACT if k % 2 == 0 else DVE, xT.t[:, k, 0:ntok], bk.f32()[:, 0:ntok], [], [bk.buf, xT.b])
                bk.put()
                acc.add(k)
            return acc

        def final_out(tl):
            ntok = tl["ntok"]
            gf = [A[2], A[3]]
            for j in range(2):
                fw.dma(SP, gf[j].t[:, :], bass.AP(W["norm_final"].ap().tensor, j * 512, [[0, 128], [1, 512]]), writes=[gf[j].b])
            small = small_f
            for gi, (goff, gtok) in enumerate(tl["groups"]):
                bks = [pp.get(), pp.get()]
                for k in range(KC):
                    tr_(bks[k // 4].f32()[0:gtok, (k % 4) * 128:(k % 4 + 1) * 128], xT.t[:, k, goff:goff + gtok], idf.t[:, :],
                        [xT.b, idf.b], [bks[k // 4].buf])
                for j in range(2):
                    act_(ysq.t[0:gtok, :], bks[j].f32()[0:gtok, :], AF.Square, [], [bks[j].buf, ysq.b, small.b],
                         accum_out=small.t[0:gtok, j:j + 1])
                tt_(DVE, small.t[0:gtok, 2:3], small.t[0:gtok, 0:1], small.t[0:gtok, 1:2], ALU.add, [], [small.b])
                act_(small.t[0:gtok, 2:3], small.t[0:gtok, 2:3], AF.Sqrt, [], [small.b], scale=1.0 / D, bias=NORM_EPS)
                fw.op(DVE, lambda e: e.reciprocal(out=small.t[0:gtok, 2:3], in_=small.t[0:gtok, 2:3]), [], [small.b])
                ysl = gi % 2
                for j in range(2):
                    stt_(yst_v[0:gtok, ysl, j * 512:(j + 1) * 512], bks[j].f32()[0:gtok, :], small.t[0:gtok, 2:3],
                         gf[j].t[0:gtok, :], ALU.mult, ALU.mult, [gf[j].b, small.b], [bks[j].buf, BIG.b])
                    bks[j].put()
                if tl["kind"] == "p":
                    dst = y_d.ap()[tl["t0"] + goff:tl["t0"] + goff + gtok, :]
                else:
                    dst = ys_d.ap()[tl["seq"]]
                fw.dma(SP, dst, yst_v[0:gtok, ysl, :], reads=[BIG.b], is_output=True)

        def projections(tl, acc=None):
            ntok = tl["ntok"]
            groups = tl["groups"]
            kind = tl["kind"]
            if acc is not None:
                acc.finish(1)
            else:
                rms_feat(1, ntok)
            for si, c0 in enumerate([0, 512, 1024, 1536]):
                w = min(512, RW - c0)
                slot = slabs.acquire("in")
                ws = slot.t[:, 0:8 * w].rearrange("p (k c) -> p k c", c=w)
                for gi, (goff, gtok) in enumerate(groups):
                    bk = pp.get()
                    for k in range(KC):
                        mm_(bk.f32()[0:gtok, 0:w], hT.t[:, k, goff:goff + gtok], ws[:, k, :], [hT.b, slot.b], [bk.buf],
                            start=(k == 0), stop=(k == KC - 1))
                    copy_(ACT if gi % 2 == 0 else DVE, pall_v[0:gtok, gi, c0:c0 + w], bk.f32()[0:gtok, 0:w], [], [bk.buf, BIG.b])
                    bk.put()
                slabs.release()
            prep0 = rwkv_prep_early(tl, 0, *groups[0])

            def pstep():
                try:
                    next(prep0)
                except StopIteration:
                    pass

            for which in range(2):
                slot = slabs.acquire("in")
                ws = slot.t[:, 0:4096].rearrange("p (k c) -> p k c", c=512)
                for j in range(4):
                    pstep()
                    bk = pp.get("att")
                    for k in range(KC):
                        mm_(bk.f32()[:, 0:ntok], ws[:, k, j * 128:(j + 1) * 128], hT.t[:, k, 0:ntok], [slot.b, hT.b], [bk.buf],
                            start=(k == 0), stop=(k == KC - 1))
                    if which == 0:
                        copy_(ACT if j % 2 == 0 else DVE, QT.t[:, j, 0:ntok], bk.f32()[:, 0:ntok], [], [bk.buf, QT.b])
                    else:
                        for gi, (goff, gtok) in enumerate(groups):
                            kt = KT[key_slot(tl, gi)]
                            copy_(ACT if (j + gi) % 2 == 0 else DVE, kt.t[:, j, 0:gtok], bk.f32()[:, goff:goff + gtok], [], [bk.buf, kt.b])
                    bk.put()
                if which == 1 and tl["last"] and "nokv" not in dbg:
                    for gi, (goff, gtok) in enumerate(groups):
                        bk = pp.get("att")
                        for k in range(KC):
                            mm_(bk.f32()[0:gtok, :], hT.t[:, k, goff:goff + gtok], ws[:, k, :], [hT.b, slot.b], [bk.buf],
                                start=(k == 0), stop=(k == KC - 1))
                        st = TMV_
                        copy_(ACT, st.t[0:gtok, :], bk.f32()[0:gtok, :], [], [bk.buf, st.b])
                        bk.put()
                        kv_store(tl, gi, gtok, st, "k")
                slabs.release()
            slot = slabs.acquire("in")
            ws = slot.t[:, 0:4096].rearrange("p (k c) -> p k c", c=512)
            for gi, (goff, gtok) in enumerate(groups):
                pstep()
                bk = pp.get("att")
                for k in range(KC):
                    mm_(bk.f32()[0:gtok, :], hT.t[:, k, goff:goff + gtok], ws[:, k, :], [hT.b, slot.b], [bk.buf],
                        start=(k == 0), stop=(k == KC - 1))
                va = vaug[key_slot(tl, gi)]
                copy_(ACT, va.t[0:gtok, :, 0:64], bk.f32()[0:gtok, :].rearrange("p (h d) -> p h d", d=64), [], [bk.buf, va.b])
                if tl["last"] and "nokv" not in dbg:
                    st = ysb2[1]
                    copy_(DVE, st.t[0:gtok, :], bk.f32()[0:gtok, :], [], [bk.buf, st.b])
                    kv_store(tl, gi, gtok, st, "v")
                bk.put()
            slabs.release()
            return prep0

        def key_slot(tl, gi):
            if tl["kind"] == "p":
                return (tl["idx"] * 4 + gi) % 8
            return 4

        def kv_store(tl, gi, gtok, st, which):
            with nc.allow_non_contiguous_dma(reason="kv cache rows (256B runs)"):
                for h in range(NH):
                    if tl["kind"] == "p":
                        d = (pk_d if which == "k" else pv_d).ap()[h, gi * 128:gi * 128 + gtok, :]
                    else:
                        d = (sk_d if which == "k" else sv_d).ap()[tl["seq"], h, :, :]
                    fw.dma(SP, d, st.t[0:gtok, h * 64:(h + 1) * 64], reads=[st.b], is_output=True)

        def rwkv_prep_early(tl, gi, goff, gtok):
            G = slice(0, gtok)
            nch = max(1, gtok // 64)
            p = pall_v[G, gi, :]
            if gi > 0:
                fw.dma(SP, xs_t.t[0:1, :], pall_v[127:128, gi - 1, :], reads=[BIG.b], writes=[XS0])
            elif tl["kind"] == "s":
                fw.dma(SP, xs_t.t[0:1, :], stsh_d.ap()[tl["seq"]:tl["seq"] + 1, :], writes=[XS0])
            elif tl["first"]:
                fw.op(POOL, lambda e: e.memset(xs_t.t[0:1, :], 0.0), writes=[XS0])
            else:
                fw.dma(SP, xs_t.t[0:1, :], carry_d.ap(), reads=[CARRY], writes=[XS0])
            if gtok == 128:
                fw.dma(SP, xs_t.t[1:113, :], pall_v[0:112, gi, :], reads=[BIG.b], writes=[xs_t.b])
                fw.dma(SP, xs_t.t[113:128, :], pall_v[112:127, gi, :], reads=[BIG.b], writes=[XS1])
            else:
                fw.dma(SP, xs_t.t[1:gtok, :], pall_v[0:gtok - 1, gi, :], reads=[BIG.b], writes=[xs_t.b])
            if gi == len(tl["groups"]) - 1:
                if tl["last"]:
                    dst = pshift_d.ap() if tl["kind"] == "p" else sshift_d.ap()[tl["seq"]:tl["seq"] + 1, :]
                    fw.dma(SP, dst, pall_v[gtok - 1:gtok, gi, :], reads=[BIG.b], is_output=True)
                else:
                    fw.dma(SP, carry_d.ap(), pall_v[gtok - 1:gtok, gi, :], reads=[BIG.b], writes=[CARRY])
            yield
            yield
            xs = xs_t.t
            XSB = [xs_t.b, XS0, XS1]
            tt_(POOL, xs[G, :], xs[G, :], p, ALU.subtract, [BIG.b], XSB)
            tt_(POOL, xs[G, :], xs[G, :], mu_b.t[G, :], ALU.mult, [mu_b.b], XSB)
            tt_(POOL, xs[G, :], xs[G, :], p, ALU.add, [BIG.b], XSB)
            yield
            yield
            r_, k_, v_ = xs[G, 0:512], xs[G, 512:1024], xs[G, 1024:1536]
            act_(lin.t[G, 0:64], xs[G, 1536:1600], AF.Tanh, XSB, [lin.b])
            act_(lin.t[G, 64:128], xs[G, 1600:1664], AF.Copy, XSB, [lin.b])
            act_(lin.t[G, 128:256], xs[G, 1664:1792], AF.Sigmoid, XSB, [lin.b])
            yield
            bk = pp.get("prep")
            for j in range(2):
                tr_(bk.bf16()[:, j * 128:j * 128 + gtok], lin.t[G, j * 128:(j + 1) * 128], idb.t[G, G], [lin.b, idb.b], [bk.buf])
            yield
            copy_(DVE, linT.t[:, :, G], bk.bf16()[:, 0:256].rearrange("p (j t) -> p j t", t=128)[:, :, G], [], [bk.buf, linT.b])
            bk.put()
            yield
            bw, ba = pp.get("prep"), pp.get("prep")
            mm_(bw.f32()[G, :], linT.t[0:64, 0, G], wal.t[0:64, :], [linT.b, wal.b], [bw.buf])
            mm_(ba.f32()[G, :], linT.t[64:128, 0, G], wal.t[64:128, :], [linT.b, wal.b], [ba.buf])
            yield
            sg, cs, e3t, e4t, a_, kkn, scr, kp, b_ = [A[i] for i in range(9)]
            tt_(DVE, sg.t[G, :], bw.f32()[G, :], cb512["w0"].t[G, :], ALU.add, [cb512["w0"].b], [bw.buf, sg.b])
            bw.put()
            tt_(DVE, a_.t[G, :], ba.f32()[G, :], cb512["a0"].t[G, :], ALU.add, [cb512["a0"].b], [ba.buf, a_.b])
            ba.put()
            yield
            act_(sg.t[G, :], sg.t[G, :], AF.Sigmoid, [], [sg.b])
            act_(a_.t[G, :], a_.t[G, :], AF.Sigmoid, [], [a_.b])
            yield
            bcs, btot = pp.get("prep"), pp.get("prep")
            mm_(bcs.f32()[G, :], tri.t[G, G], sg.t[G, :], [tri.b, sg.b], [bcs.buf])
            mm_(btot.f32()[G, :], bo.t[G, G], sg.t[G, :], [bo.b, sg.b], [btot.buf])
            yield
            copy_(ACT, cs.t[G, :], bcs.f32()[G, :], [], [bcs.buf, cs.b])
            bcs.put()
            yield
            tt_(DVE, e3t.t[G, :], cs.t[G, :], sg.t[G, :], ALU.subtract, [cs.b, sg.b], [e3t.b])
            tt_(DVE, e4t.t[G, :], btot.f32()[G, :], cs.t[G, :], ALU.subtract, [cs.b], [btot.buf, e4t.b])
            btot.put()
            bwc = pp.get("prep")
            wc = wc2.t[:, gi % 2]
            for h in range(NH):
                mm_(bwc.f32()[0:64, h * 2:h * 2 + nch], sg.t[G, h * 64:(h + 1) * 64], ci.t[G, 0:nch], [sg.b, ci.b], [bwc.buf])
            yield
            act_(wc[:, :, 0:nch], bwc.f32()[0:64, 0:16].rearrange("p (a c) -> p a c", c=2)[:, :, 0:nch], AF.Exp, [], [bwc.buf, wc2.b], scale=-C0)
            bwc.put()
            act_(e3t.t[G, :], e3t.t[G, :], AF.Exp, [], [e3t.b], scale=-C0)
            act_(e4t.t[G, :], e4t.t[G, :], AF.Exp, [], [e4t.b], scale=-C0)
            v3 = lambda ap: ap.rearrange("p (h n) -> p h n", n=64)
            sp_ = small_p
            tt_(POOL, kkn.t[G, :], k_, cb512["k_k"].t[G, :], ALU.mult, XSB + [cb512["k_k"].b], [kkn.b])
            tt_(POOL, scr.t[G, :], kkn.t[G, :], kkn.t[G, :], ALU.mult, [kkn.b], [scr.b])
            yield
            yield
            fw.op(DVE, lambda e: e.tensor_reduce(out=sp_.t[G, 0:8], in_=v3(scr.t[G, :]), axis=AX.X, op=ALU.add), [scr.b], [sp_.b])
            ts_(DVE, sp_.t[G, 0:8], sp_.t[G, 0:8], 1e-24, ALU.max, [], [sp_.b])
            stt_(scr.t[G, :], a_.t[G, :], -1.0, cb512["k_a"].t[G, :], ALU.add, ALU.mult, [a_.b, cb512["k_a"].b], [scr.b])
            stt_(kp.t[G, :], scr.t[G, :], 1.0, k_, ALU.add, ALU.mult, [scr.b] + XSB, [kp.b])
            yield
            act_(sp_.t[G, 0:8], sp_.t[G, 0:8], AF.Sqrt, [], [sp_.b])
            yield
            fw.op(DVE, lambda e: e.reciprocal(out=sp_.t[G, 8:16], in_=sp_.t[G, 0:8]), [], [sp_.b])
            tt_(DVE, v3(kkn.t[G, :]), v3(kkn.t[G, :]), sp_.t[G, 8:16].unsqueeze(2).to_broadcast([gtok, 8, 64]), ALU.mult,
                [sp_.b], [kkn.b])
            yield
            tt_(POOL, b_.t[G, :], a_.t[G, :], kkn.t[G, :], ALU.mult, [a_.b, kkn.b], [b_.b])
            tt_(POOL, scr.t[G, :], r_, kp.t[G, :], ALU.mult, XSB + [kp.b], [scr.b])
            tt_(POOL, scr.t[G, :], scr.t[G, :], cb512["r_k"].t[G, :], ALU.mult, [cb512["r_k"].b], [scr.b])
            bg = pp.get("prep")
            mm_(bg.f32()[G, :], linT.t[:, 1, G], gl.t[:, :], [linT.b, gl.b], [bg.buf])
            yield
            yield
            bonus, gsb = bonus2[gi % 2], gsb2[gi % 2]
            fw.op(DVE, lambda e: e.tensor_reduce(out=sp_.t[G, 16:24], in_=v3(scr.t[G, :]), axis=AX.X, op=ALU.add), [scr.b], [sp_.b])
            tt_(DVE, v3(bonus.t[G, :]), v3(v_), sp_.t[G, 16:24].unsqueeze(2).to_broadcast([gtok, 8, 64]), ALU.mult,
                XSB + [sp_.b], [bonus.b])
            copy_(ACT, gsb.t[G, :], bg.f32()[G, :], [], [bg.buf, gsb.b])
            bg.put()
            yield
            act_(scr.t[G, :], cs.t[G, :], AF.Exp, [cs.b], [scr.b], scale=-C0)
            act_(cs.t[G, :], cs.t[G, :], AF.Exp, [], [cs.b], scale=C0)
            yield

        def rwkv_prep_late(tl, gi, goff, gtok):
            G = slice(0, gtok)
            nch = max(1, gtok // 64)
            C = gtok // nch
            xs = xs_t.t
            XSB = [xs_t.b, XS0, XS1]
            r_, k_, v_ = xs[G, 0:512], xs[G, 512:1024], xs[G, 1024:1536]
            sg, cs, e3t, e4t, a_, kkn, scr, kp, b_ = [A[i] for i in range(9)]
            v3 = lambda ap: ap.rearrange("p (h n) -> p h n", n=64)
            tm = TMA.t
            tt_(DVE, tm[G, 4, :], b_.t[G, :], cs.t[G, :], ALU.mult, [b_.b, cs.b], [TMAb[4]])
            tt_(POOL, tm[G, 6, :], r_, scr.t[G, :], ALU.mult, XSB + [scr.b], [TMAb[6]])
            tt_(DVE, tm[G, 5, :], kp.t[G, :], cs.t[G, :], ALU.mult, [kp.b, cs.b], [TMAb[5]])
            tt_(DVE, tm[G, 0, :], kkn.t[G, :], e3t.t[G, :], ALU.mult, [kkn.b, e3t.b], [TMAb[0]])
            copy_(ACT, tm[G, 3, :], v_, XSB, [TMAb[3]])
            tt_(POOL, tm[G, 1, :], b_.t[G, :], e4t.t[G, :], ALU.mult, [b_.b, e4t.b], [TMAb[1]])
            tt_(DVE, tm[G, 2, :], kp.t[G, :], e4t.t[G, :], ALU.mult, [kp.b, e4t.b], [TMAb[2]])
            for (tdst, idxs) in [(T1, (4, 5)), (T2, (0, 6))]:
                for wi, idx in enumerate(idxs):
                    bk = pp.get("prep")
                    for h in range(NH):
                        tr_(bk.bf16()[0:64, h * 128:h * 128 + gtok], tm[G, idx, h * 64:(h + 1) * 64], idb.t[G, G],
                            [TMAb[idx], idb.b], [bk.buf])
                    src = bk.bf16()[0:64, :].rearrange("p (a c t) -> p a c t", a=8, c=2)[:, :, 0:nch, 0:C]
                    copy_(ACT if wi == 0 else DVE, tdst.t[:, :, 0:nch, wi, 0:C], src, [], [bk.buf, tdst.b])
                    bk.put()

        def rwkv_chunks(tl, gi, chunks, fill):
            tm = TMA.t
            wc = wc2.t[:, gi % 2]
            ysb = ysb2[gi % 2]
            nck = len(chunks)
            C = chunks[0][2]
            Cc = slice(0, C)
            PA = slice(0, (nck - 1) * 64 + C)
            PPs = [slice(pb, pb + C) for (_, pb, _) in chunks]

            def hview(bank):
                return bank.f32().rearrange("p (h c) -> p h c", c=64)

            def bc(m, nh):
                return msk[m].t[PA, Cc].unsqueeze(1).to_broadcast([PA.stop, nh, C])

            for half in range(2):
                hs = slice(half * 4, half * 4 + 4)
                bB = pp.get("chunk")
                vB = bB.f32().rearrange("p (h w c) -> p h w c", h=4, w=2)
                for (cidx, pb, _), Pp in zip(chunks, PPs):
                    for h in range(half * 4, half * 4 + 4):
                        mm_(vB[Pp, h % 4, :, Cc], T1.t[:, h, cidx, 0, Cc], T2.t[:, h, cidx, :, Cc], [T1.b, T2.b], [bB.buf])
                tt_(DVE, cm["A"].t[PA, hs, Cc], vB[PA, :, 0, Cc], bc("NSU", 4), ALU.mult, [msk["NSU"].b], [bB.buf, cm["A"].b])
                tt_(DVE, cm["RBT"].t[PA, hs, Cc], vB[PA, :, 1, Cc], bc("UI", 4), ALU.mult, [msk["UI"].b], [bB.buf, cm["RBT"].b])
                bB.put()
                bK = pp.get("chunk")
                vK = bK.f32().rearrange("p (h w c) -> p h w c", h=4, w=2)
                for (cidx, pb, _), Pp in zip(chunks, PPs):
                    for h in range(half * 4, half * 4 + 4):
                        mm_(vK[Pp, h % 4, :, Cc], T1.t[:, h, cidx, 1, Cc], T2.t[:, h, cidx, :, Cc], [T1.b, T2.b], [bK.buf])
                tt_(DVE, cm["MT"].t[PA, hs, Cc], vK[PA, :, 0, Cc], bc("SU", 4), ALU.mult, [msk["SU"].b], [bK.buf, cm["MT"].b])
                tt_(DVE, cm["RKT"].t[PA, hs, Cc], vK[PA, :, 1, Cc], bc("UI", 4), ALU.mult, [msk["UI"].b], [bK.buf, cm["RKT"].b])
                bK.put()
                fill()
            bL = pp.get("chunk")
            for (cidx, pb, _), Pp in zip(chunks, PPs):
                for h in range(NH):
                    mm_(hview(bL)[Pp, h, Cc], T2.t[:, h, cidx, 0, Cc], T1.t[:, h, cidx, 0, Cc], [T1.b, T2.b], [bL.buf])
            tt_(DVE, cm["AT"].t[PA, :, Cc], hview(bL)[PA, :, Cc], bc("NSL", 8), ALU.mult, [msk["NSL"].b], [bL.buf, cm["AT"].b])
            bL.put()
            tt_(DVE, cm["P0"].t[PA, :, Cc], cm["A"].t[PA, :, Cc], bc("IDR", 8), ALU.add, [cm["A"].b, msk["IDR"].b], [cm["P0"].b])
            fill()

            def lockstep(mm_fn, evac_fn):
                banks = [pp.get("chunk") for _ in chunks]
                for h in range(NH):
                    for ci in range(nck):
                        mm_fn(ci, h, banks[ci])
                CHUNK_FREE[0] = False
                fill()
                CHUNK_FREE[0] = True
                for ci in range(nck):
                    evac_fn(ci, banks[ci])
                    banks[ci].put()
                fill()

            X, XT, Pm = cm["A"], cm["AT"], cm["P0"]
            nlev = max(1, int(math.ceil(math.log2(C))) - 1)
            for lvl in range(nlev):
                Xn = cm["X0"] if lvl % 2 == 0 else cm["X1"]
                XTn = cm["XT0"] if lvl % 2 == 0 else cm["XT1"]
                Pn = cm["P1"] if lvl % 2 == 0 else cm["P0"]
                lockstep(lambda ci, h, bk: mm_(hview(bk)[PPs[ci], h, Cc], X.t[PPs[ci], h, Cc], XT.t[PPs[ci], h, Cc], [X.b, XT.b], [bk.buf]),
                         lambda ci, bk: copy_(ACT if ci == 0 else DVE, XTn.t[PPs[ci], :, Cc], hview(bk)[PPs[ci], :, Cc], [], [bk.buf, XTn.b]))
                if lvl < nlev - 1:
                    lockstep(lambda ci, h, bk: mm_(hview(bk)[PPs[ci], h, Cc], XT.t[PPs[ci], h, Cc], X.t[PPs[ci], h, Cc], [X.b, XT.b], [bk.buf]),
                             lambda ci, bk: copy_(DVE if ci == 0 else ACT, Xn.t[PPs[ci], :, Cc], hview(bk)[PPs[ci], :, Cc], [], [bk.buf, Xn.b]))
                lockstep(lambda ci, h, bk: mm_(hview(bk)[PPs[ci], h, Cc], XTn.t[PPs[ci], h, Cc], Pm.t[PPs[ci], h, Cc], [XTn.b, Pm.b], [bk.buf]),
                         lambda ci, bk: tt_(DVE, Pn.t[PPs[ci], :, Cc], hview(bk)[PPs[ci], :, Cc], Pm.t[PPs[ci], :, Cc], ALU.add, [Pm.b], [bk.buf, Pn.b]))
                X, XT, Pm = Xn, XTn, Pn
            TTm = Pm
            lockstep(lambda ci, h, bk: mm_(hview(bk)[0:64, h, Cc], tm[PPs[ci], 0, h * 64:(h + 1) * 64], TTm.t[PPs[ci], h, Cc], [TMAb[0], TTm.b], [bk.buf]),
                     lambda ci, bk: copy_(ACT, TKT2.t[:, ci, :, Cc], hview(bk)[0:64, :, Cc], [], [bk.buf, TKT2.b]))
            lockstep(lambda ci, h, bk: mm_(hview(bk)[PPs[ci], h, :], cm["MT"].t[PPs[ci], h, Cc], tm[PPs[ci], 3, h * 64:(h + 1) * 64], [cm["MT"].b, TMAb[3]], [bk.buf]),
                     lambda ci, bk: copy_(DVE if ci == 0 else ACT, cm["MV"].t[PPs[ci], :, :], hview(bk)[PPs[ci], :, :], [], [bk.buf, cm["MV"].b]))
            lockstep(lambda ci, h, bk: mm_(hview(bk)[PPs[ci], h, :], TTm.t[PPs[ci], h, Cc], cm["MV"].t[PPs[ci], h, :], [TTm.b, cm["MV"].b], [bk.buf]),
                     lambda ci, bk: copy_(ACT if ci == 0 else DVE, TMV.t[PPs[ci], :, :], hview(bk)[PPs[ci], :, :], [], [bk.buf, TMV.b]))
            for ci, ((cidx, pb, _), Pp) in enumerate(zip(chunks, PPs)):
                tt_(DVE, Stmp.t[:, :, :], S32.t[:, :, :], wc[:, :, cidx:cidx + 1].to_broadcast([64, NH, 64]), ALU.mult, [S32.b, wc2.b], [Stmp.b])
                bu = pp.get("chunk")
                for h in range(NH):
                    mm_(hview(bu)[Pp, h, :], TKT2.t[:, ci, h, Cc], Sbf.t[:, h, :], [TKT2.b, Sbf.b], [bu.buf])
                by1 = pp.get("chunk")
                for h in range(NH):
                    mm_(hview(by1)[Pp, h, :], T2.t[:, h, cidx, 1, Cc], Sbf.t[:, h, :], [T2.b, Sbf.b], [by1.buf])
                stt_(cm["U"].t[Pp, :, :], hview(bu)[Pp, :, :], -1.0, TMV.t[Pp, :, :], ALU.mult, ALU.subtract, [TMV.b], [bu.buf, cm["U"].b])
                bu.put()
                copy_(ACT, ysb.t[Pp, :], by1.f32()[Pp, :], [], [by1.buf, ysb.b])
                by1.put()
                fill()
                bs = pp.get("chunk")
                vs = hview(bs)
                for h in range(NH):
                    mm_(vs[0:64, h, :], tm[Pp, 2, h * 64:(h + 1) * 64], tm[Pp, 3, h * 64:(h + 1) * 64], [TMAb[2], TMAb[3]], [bs.buf], start=True, stop=False)
                    mm_(vs[0:64, h, :], tm[Pp, 1, h * 64:(h + 1) * 64], cm["U"].t[Pp, h, :], [TMAb[1], cm["U"].b], [bs.buf], start=False, stop=True)
                by2 = pp.get("chunk")
                vy2 = hview(by2)
                for h in range(NH):
                    mm_(vy2[Pp, h, :], cm["RKT"].t[Pp, h, Cc], tm[Pp, 3, h * 64:(h + 1) * 64], [cm["RKT"].b, TMAb[3]], [by2.buf], start=True, stop=False)
                    mm_(vy2[Pp, h, :], cm["RBT"].t[Pp, h, Cc], cm["U"].t[Pp, h, :], [cm["RBT"].b, cm["U"].b], [by2.buf], start=False, stop=True)
                tt_(DVE, Sbf.t[:, :, :], Stmp.t[:, :, :], vs[0:64, :, :], ALU.add, [Stmp.b], [bs.buf, Sbf.b])
                tt_(DVE, S32.t[:, :, :], Stmp.t[:, :, :], vs[0:64, :, :], ALU.add, [Stmp.b], [bs.buf, S32.b])
                bs.put()
                tt_(DVE, ysb.t[Pp, :], ysb.t[Pp, :], by2.f32()[Pp, :], ALU.add, [], [by2.buf, ysb.b])
                by2.put()
                fill()

        def rwkv_out(tl, gi, gtok):
            G = slice(0, gtok)
            ysb, bonus, gsb, mix = ysb2[gi % 2], bonus2[gi % 2], gsb2[gi % 2], mix2[gi % 2]
            v3 = lambda ap: ap.rearrange("p (h n) -> p h n", n=64)
            sm = small_o.t
            SO = small_o.b
            tt_(POOL, ysq.t[G, :], ysb.t[G, :], ysb.t[G, :], ALU.mult, [ysb.b], [ysq.b])
            fw.op(DVE, lambda e: e.tensor_reduce(out=sm[G, 0:8], in_=v3(ysb.t[G, :]), axis=AX.X, op=ALU.add), [ysb.b], [SO])
            ts_(DVE, sm[G, 0:8], sm[G, 0:8], 1.0 / 64, ALU.mult, [], [SO])
            tt_(DVE, sm[G, 16:24], sm[G, 0:8], sm[G, 0:8], ALU.mult, [], [SO])
            yield
            yield
            fw.op(DVE, lambda e: e.tensor_reduce(out=sm[G, 8:16], in_=v3(ysq.t[G, :]), axis=AX.X, op=ALU.add), [ysq.b], [SO])
            stt_(sm[G, 8:16], sm[G, 8:16], 1.0 / 64, sm[G, 16:24], ALU.mult, ALU.subtract, [], [SO])
            ts_(DVE, sm[G, 8:16], sm[G, 8:16], 0.0, ALU.max, [], [SO])
            tt_(DVE, v3(ysb.t[G, :]), v3(ysb.t[G, :]), sm[G, 0:8].unsqueeze(2).to_broadcast([gtok, 8, 64]), ALU.subtract, [SO], [ysb.b])
            yield
            act_(sm[G, 8:16], sm[G, 8:16], AF.Sqrt, [], [SO], bias=GN_EPS)
            yield
            fw.op(DVE, lambda e: e.reciprocal(out=sm[G, 8:16], in_=sm[G, 8:16]), [], [SO])
            tt_(DVE, v3(ysb.t[G, :]), v3(ysb.t[G, :]), sm[G, 8:16].unsqueeze(2).to_broadcast([gtok, 8, 64]), ALU.mult, [SO], [ysb.b])
            tt_(DVE, ysb.t[G, :], ysb.t[G, :], cb512["ln_x_w"].t[G, :], ALU.mult, [cb512["ln_x_w"].b], [ysb.b])
            yield
            tt_(DVE, ysb.t[G, :], ysb.t[G, :], cb512["ln_x_b"].t[G, :], ALU.add, [cb512["ln_x_b"].b], [ysb.b])
            tt_(DVE, ysb.t[G, :], ysb.t[G, :], bonus.t[G, :], ALU.add, [bonus.b], [ysb.b])
            tt_(DVE, mix.t[G, 0:512], ysb.t[G, :], gsb.t[G, :], ALU.mult, [gsb.b, ysb.b], [mix.b])
            yield

        def attention(tl, gi, goff, nq):
            Q = slice(0, nq)
            mix = mix2[gi % 2]
            if tl["kind"] == "p":
                qb = tl["idx"] * 4 + gi
                roles = [(r, qb - 4 + r) for r in range(5) if qb - 4 + r >= 0]
                slot_of = lambda kb: kb % 8
                nk_own = 128
                corners = True
            else:
                roles = [(r, r) for r in range(5)]
                slot_of = lambda kb: kb
                nk_own = nq
                corners = False
            bO = [pp.get("att"), pp.get("att")]
            for h in range(NH):
                hb, hp = 64 * (h % 2), h // 2
                HB = slice(hb, hb + 64)
                pt = PT[0]
                bA, bB_ = pp.get("att"), pp.get("att")
                for (r, kb) in roles:
                    nk = nk_own if r == 4 else 128
                    kt = KT[slot_of(kb)]
                    dst = (bA.f32()[0:nk, r * 128:r * 128 + nq] if r < 4 else bB_.f32()[0:nk, 0:nq])
                    bnk = bA if r < 4 else bB_
                    mm_(dst, kt.t[HB, hp, 0:nk], QT.t[HB, hp, goff:goff + nq], [kt.b, QT.b], [bnk.buf], start=True, stop=(r < 3))
                    if r == 3:
                        mm_(dst, idb.t[:, :], T3b.t[:, h, Q], [idb.b, T3b.b], [bnk.buf], start=False, stop=True)
                    if r == 4:
                        mm_(dst, idb.t[:, 0:nk], T4b.t[:, h, Q], [idb.b, T4b.b], [bnk.buf], start=False, stop=True)
                yield
                for (r, kb) in roles:
                    nk = nk_own if r == 4 else 128
                    bnk = bA if r < 4 else bB_
                    src = (bA.f32()[:, r * 128:r * 128 + nq] if r < 4 else bB_.f32()[:, 0:nq])
                    kw = dict(scale=0.125)
                    rd = [cbias.b] if r < 3 else []
                    if corners and r == 0:
                        act_(pt.t[64:128, r, Q], src[64:128, :], AF.Exp, rd, [bnk.buf, pt.b], scale=0.125, bias=cbias.t[64:128, h:h + 1])
                        act_(pt.t[0:64, r, 0:64], src[0:64, 0:64], AF.Exp, rd, [bnk.buf, pt.b], scale=0.125, bias=cbias.t[0:64, h:h + 1])
                    elif corners and r == 4:
                        act_(pt.t[0:64, r, Q], src[0:64, :], AF.Exp, rd, [bnk.buf, pt.b], **kw)
                        act_(pt.t[64:128, r, 64:128], src[64:128, 64:128], AF.Exp, rd, [bnk.buf, pt.b], **kw)
                    else:
                        if r < 3:
                            kw["bias"] = cbias.t[0:nk, h:h + 1]
                        act_(pt.t[0:nk, r, Q], src[0:nk, :], AF.Exp, rd, [bnk.buf, pt.b], **kw)
                bA.put()
                bB_.put()
                yield
                vo = bO[h // 4].f32()[:, 0:260].rearrange("p (a c) -> p a c", c=65)
                for i, (r, kb) in enumerate(roles):
                    nk = nk_own if r == 4 else 128
                    va = vaug[slot_of(kb)]
                    mm_(vo[Q, h % 4, :], pt.t[0:nk, r, Q], va.t[0:nk, h, :], [pt.b, va.b], [bO[h // 4].buf],
                        start=(i == 0), stop=(i == len(roles) - 1))
                yield
            for half in range(2):
                vo = bO[half].f32()[:, 0:260].rearrange("p (a c) -> p a c", c=65)
                fw.op(DVE, lambda e: e.reciprocal(out=small_a.t[Q, half * 4:half * 4 + 4], in_=vo[Q, :, 64]), [], [bO[half].buf, small_a.b])
                tt_(DVE, mix.t[Q, 512 + half * 256:768 + half * 256].rearrange("p (a c) -> p a c", c=64), vo[Q, :, 0:64],
                    small_a.t[Q, half * 4:half * 4 + 4].unsqueeze(2).to_broadcast([nq, 4, 64]), ALU.mult, [small_a.b], [bO[half].buf, mix.b])
                bO[half].put()
            yield

        CHUNK_FREE = [True]

        def drain(g):
            for _ in g:
                pass

        def mixer(tl, prep0=None):
            groups = tl["groups"]
            ng = len(groups)
            drain(prep0 if prep0 is not None else rwkv_prep_early(tl, 0, *groups[0]))
            rwkv_prep_late(tl, 0, *groups[0])
            gens = []

            rr = [0]

            def fill():
                if not gens:
                    return
                rr[0] += 1
                for i, g in enumerate(list(gens)):
                    if len(gens) > 1 and (i + rr[0]) % 2 == 0:
                        continue
                    try:
                        next(g)
                    except StopIteration:
                        gens.remove(g)

            def post(gi, goff, gtok):
                yield from rwkv_out(tl, gi, gtok)
                while not CHUNK_FREE[0]:
                    yield
                dbg_dump("mix", mix2[gi % 2].t[0:gtok, :], [gtok, 1024], [mix2[gi % 2].b])
                mix_transpose(gi, goff, gtok)
                yield

            for gi, (goff, gtok) in enumerate(groups):
                att = attention(tl, gi, goff, gtok)
                gens.append(att)
                if gi + 1 < ng:
                    gens.append(rwkv_prep_early(tl, gi + 1, *groups[gi + 1]))
                nch = max(1, gtok // 64)
                C = gtok // nch
                rwkv_chunks(tl, gi, [(cidx, cidx * 64, C) for cidx in range(nch)], fill)
                while gens:
                    fill()
                if gi + 1 < ng:
                    rwkv_prep_late(tl, gi + 1, *groups[gi + 1])
                    gens.append(post(gi, goff, gtok))
                else:
                    drain(post(gi, goff, gtok))

        def mix_transpose(gi, goff, gtok):
            G = slice(0, gtok)
            mix = mix2[gi % 2]
            bk = pp.get("chunk")
            for m in range(8):
                tr_(bk.bf16()[:, m * 128:m * 128 + gtok], mix.t[G, m * 128:(m + 1) * 128], idb.t[G, G], [mix.b, idb.b], [bk.buf])
            copy_(ACT, hT.t[:, :, goff:goff + gtok], bk.bf16().rearrange("p (m t) -> p m t", t=128)[:, :, G], [], [bk.buf, hT.b])
            bk.put()

        def out_proj(ntok):
            acc = NormAcc(ntok)
            for s in range(2):
                slot = slabs.acquire("out")
                ws = slot.t[:, 0:4096].rearrange("p (k c) -> p k c", c=512)
                for j in range(4):
                    dc = 4 * s + j
                    bk = pp.get()
                    for m in range(KC):
                        mm_(bk.f32()[:, 0:ntok], ws[:, m, j * 128:(j + 1) * 128], hT.t[:, m, 0:ntok], [slot.b, hT.b], [bk.buf],
                            start=(m == 0), stop=(m == KC - 1))
                    tt_(DVE, xT.t[:, dc, 0:ntok], bk.f32()[:, 0:ntok], xT.t[:, dc, 0:ntok], ALU.add, [], [bk.buf, xT.b])
                    bk.put()
                    acc.add(dc)
                slabs.release()
            return acc

        def state_zero():
            fw.op(POOL, lambda e: e.memset(S32.t[:], 0.0), writes=[S32.b])
            fw.op(POOL, lambda e: e.memset(Sbf.t[:], 0.0), writes=[Sbf.b])

        def state_load(seq):
            st = A[8]
            with nc.allow_non_contiguous_dma(reason="state load 256B runs"):
                for h in range(NH):
                    fw.dma(SP, st.t[0:64, h * 64:(h + 1) * 64], stwkv_d.ap()[seq, h, :, :], writes=[st.b])
            bk = pp.get()
            vs = bk.f32().rearrange("p (h c) -> p h c", c=64)
            for h in range(NH):
                tr_(vs[0:64, h, :], st.t[0:64, h * 64:(h + 1) * 64], idf.t[0:64, 0:64], [st.b, idf.b], [bk.buf])
            copy_(DVE, S32.t[:, :, :], vs[0:64, :, :], [], [bk.buf, S32.b])
            copy_(ACT, Sbf.t[:, :, :], vs[0:64, :, :], [], [bk.buf, Sbf.b])
            bk.put()

        def state_store(dst_ap):
            st = A[8]
            bk = pp.get()
            vs = bk.f32().rearrange("p (h c) -> p h c", c=64)
            for h in range(NH):
                tr_(vs[0:64, h, :], S32.t[:, h, :], idf.t[0:64, 0:64], [S32.b, idf.b], [bk.buf])
            copy_(DVE, st.t[0:64, :], bk.f32()[0:64, :], [], [bk.buf, st.b])
            bk.put()
            with nc.allow_non_contiguous_dma(reason="state store 256B runs"):
                for h in range(NH):
                    fw.dma(SP, dst_ap[h, :, :], st.t[0:64, h * 64:(h + 1) * 64], reads=[st.b], is_output=True)

        def sample_cache_load(seq):
            with nc.allow_non_contiguous_dma(reason="cache rows 256B runs"):
                for blk in range(4):
                    for h in range(NH):
                        fw.dma(SP, A[blk].t[:, h * 64:(h + 1) * 64], ck_d.ap()[seq, h, blk * 128:(blk + 1) * 128, :], writes=[A[blk].b])
                        fw.dma(SP, A[4 + blk].t[:, h * 64:(h + 1) * 64], cv_d.ap()[seq, h, blk * 128:(blk + 1) * 128, :], writes=[A[4 + blk].b])
            for blk in range(4):
                kb_t = (bonus, gsb)[blk // 2]
                kbf = kb_t.t[:].bitcast(BF16)[:, (blk % 2) * 512:(blk % 2 + 1) * 512]
                copy_(DVE, kbf, A[blk].t[:, :], [A[blk].b], [kb_t.b])
                copy_(ACT, vaug[blk].t[:, :, 0:64], A[4 + blk].t[:, :].rearrange("p (h d) -> p h d", d=64), [A[4 + blk].b], [vaug[blk].b])
                bk = pp.get()
                for hp in range(4):
                    tr_(bk.bf16()[:, hp * 128:(hp + 1) * 128], kbf[:, hp * 128:(hp + 1) * 128], idb.t[:, :], [kb_t.b, idb.b], [bk.buf])
                copy_(DVE, KT[blk].t[:, :, :], bk.bf16()[:, 0:512].rearrange("p (a t) -> p a t", t=128), [], [bk.buf, KT[blk].b])
                bk.put()

        state_zero()
        for ti, tl in enumerate(tiles):
            ntok = tl["ntok"]
            if not tl.get("xloaded"):
                issue_x_load(tl)
                tl["xloaded"] = True
            if tl["kind"] == "s":
                state_load(tl["seq"])
            acc0 = load_x(tl)
            if ti == 0:
                FLAG = Buf("flag")
                fw.op(DVE, lambda e: e.memset(small_f.t[0:1, 3:4], 0.0), reads=[xT.b], writes=[FLAG])
                convert_weights(["w_out", "w_ff2_gate", "w_ff2_up", "w_ff2_down"], after=[FLAG])
            if tl["kind"] == "s":
                sample_cache_load(tl["seq"])
            acc1 = ffn(0, ntok, acc0, want_acc=(LVL > 1))
            dbg_dump("x1", xT.t[:, :, 0:ntok], [128, KC, ntok], [xT.b])
            if LVL <= 1:
                final_out(tl)
                continue
            prep0 = projections(tl, acc1)
            dbg_dump("pall", pall_v[0:tl["groups"][0][1], 0:len(tl["groups"]), :], [tl["groups"][0][1], len(tl["groups"]), RW], [BIG.b])
            if LVL <= 2:
                final_out(tl)
                continue
            mixer(tl, prep0)
            if ti + 1 < len(tiles):
                issue_x_load(tiles[ti + 1])
                tiles[ti + 1]["xloaded"] = True
            if LVL <= 7:
                final_out(tl)
                continue
            acc2 = out_proj(ntok)
            dbg_dump("x2", xT.t[:, :, 0:ntok], [128, KC, ntok], [xT.b])
            ffn(2, ntok, acc2)
            final_out(tl)
            if tl["last"]:
                state_store(pwkv_d.ap() if tl["kind"] == "p" else swkv_d.ap()[tl["seq"]])
        fw.finish()
        info = dict(sbuf_bytes=sb_bytes[0], n_ops=fw.n_ops, n_dma=fw.n_dma, waits=[e.nwaits for e in fw.engs])
    return nc, dbg_outs, info


_CACHE = {}


def _run(inputs, n_cores, T, dbg=None, trace=False, **bkw):
    key = (T, tuple(sorted(dbg)) if dbg else None, tuple(sorted(bkw.items())))
    if key not in _CACHE:
        _CACHE[key] = build_program(T, dbg=dbg, **bkw)
    nc, dbg_outs, info = _CACHE[key]
    f = lambda a: np.ascontiguousarray(np.asarray(a, dtype=np.float32))
    in_maps = []
    for c in range(n_cores):
        m = {
            "x": f(inputs["x_prompt"][c]),
            "xs": f(inputs["x_sample"][2 * c:2 * c + 2]),
            "st_shift": f(inputs["state_shift"][0, 2 * c:2 * c + 2]),
            "st_wkv": f(inputs["state_wkv"][0, 2 * c:2 * c + 2]),
            "ck": f(inputs["cache_attn_k"][0, 2 * c:2 * c + 2]),
            "cv": f(inputs["cache_attn_v"][0, 2 * c:2 * c + 2]),
        }
        for n in WEIGHT_NAMES:
            a = np.asarray(inputs[n], dtype=np.float32)
            if n != "norm_final":
                a = a[0]
            m[n] = np.ascontiguousarray(a.reshape(WEIGHT_SHAPES[n]))
        in_maps.append(m)
    res = run_bass_kernel_spmd(nc, in_maps, core_ids=list(range(n_cores)), **({"trace": True} if trace else {}))
    return res, info


def kernel(**inputs):
    B, T = inputs["x_prompt"].shape[0], inputs["x_prompt"].shape[1]
    res, _ = _run(inputs, B, T)
    R = res.results
    cat = lambda n: np.stack([r[n] for r in R], 0)
    y_prompt = cat("y")
    y_sample = np.concatenate([r["ys"] for r in R], 0)
    p_shift = np.concatenate([r["p_shift"] for r in R], 0)[None]
    p_wkv = cat("p_wkv")[None]
    p_k = cat("p_k")[None]
    p_v = cat("p_v")[None]
    s_shift = np.concatenate([r["s_shift"] for r in R], 0)[None]
    s_wkv = np.concatenate([r["s_wkv"] for r in R], 0)[None]
    s_k = np.concatenate([r["s_k"] for r in R], 0)[None]
    s_v = np.concatenate([r["s_v"] for r in R], 0)[None]
    outs = (y_prompt, y_sample, p_shift, p_wkv, p_k, p_v, s_shift, s_wkv, s_k, s_v)
    return tuple(np.ascontiguousarray(o, dtype=np.float32) for o in outs)
```
